# Optimizing a Trainium2 kernel written in Bass

```python
import math
import jax, jax.numpy as jnp
from jax import lax
import numpy as np

D_MODEL = 1024
BATCH = 8
SEQ = 4096
DEPTH = 2

D_MIX = D_MODEL
SSD_HEADS = 8
SSD_HEAD_DIM = 64
SSD_INNER = SSD_HEADS * SSD_HEAD_DIM
SSD_GROUPS = 2
SSD_STATE = 128
SSD_CONV = 4
SSD_CHUNK = 128
SSD_XBC = SSD_INNER + 2 * SSD_GROUPS * SSD_STATE
ATT_Q_HEADS = 4
ATT_KV_HEADS = 2
ATT_HEAD_DIM = 64
ATT_WIDTH = ATT_Q_HEADS * ATT_HEAD_DIM
ATT_KV_WIDTH = ATT_KV_HEADS * ATT_HEAD_DIM
WINDOW = 128
REL_BUCKETS = 32
REL_MAX_DIST = 128
GM_GROUPS = 4
GM_GROUP_DIM = 64
GM_WIDTH = GM_GROUPS * GM_GROUP_DIM
GM_CHUNK = 128
D_FF = 2816
EPS = 1e-6

OFF_Z = 0
OFF_XBC = OFF_Z + SSD_INNER
OFF_DT = OFF_XBC + SSD_XBC
OFF_Q = OFF_DT + SSD_HEADS
OFF_K = OFF_Q + ATT_WIDTH
OFF_V = OFF_K + ATT_KV_WIDTH
OFF_U = OFF_V + ATT_KV_WIDTH
OFF_GV = OFF_U + GM_WIDTH
D_IN_PROJ = OFF_GV + GM_WIDTH

kernel_name = "hybrid_ssd_swa_sgu_macaron"


def rmsnorm(x, w):
    xf = x.astype(jnp.float32)
    y = xf * lax.rsqrt(jnp.mean(xf * xf, axis=-1, keepdims=True) + EPS)
    return (y * w.astype(jnp.float32)).astype(x.dtype)


def layernorm(x, w, b):
    xf = x.astype(jnp.float32)
    mu = jnp.mean(xf, axis=-1, keepdims=True)
    xc = xf - mu
    y = xc * lax.rsqrt(jnp.mean(xc * xc, axis=-1, keepdims=True) + EPS)
    return (y * w.astype(jnp.float32) + b.astype(jnp.float32)).astype(x.dtype)


def swiglu(x, w_gate, w_up, w_down):
    return (jax.nn.silu(x @ w_gate) * (x @ w_up)) @ w_down


def causal_depthwise_conv(x, w, b):
    c = x.shape[-1]
    y = lax.conv_general_dilated(
        x, w.astype(x.dtype)[:, None, :], window_strides=(1,),
        padding=[(SSD_CONV - 1, 0)], dimension_numbers=("NWC", "WIO", "NWC"),
        feature_group_count=c)
    return y + b.astype(x.dtype)


def ssd_mixer(z, xbc_raw, dt_raw, conv_w, conv_b, dt_bias, a_log, d_skip, norm_w):
    bsz, s, _ = z.shape
    L, G, N, P = SSD_CHUNK, SSD_GROUPS, SSD_STATE, SSD_HEAD_DIM
    hpg = SSD_HEADS // G
    nc = s // L
    xbc = jax.nn.silu(causal_depthwise_conv(xbc_raw, conv_w, conv_b))
    x = xbc[..., :SSD_INNER].reshape(bsz, nc, L, G, hpg, P)
    bm = xbc[..., SSD_INNER:SSD_INNER + G * N].reshape(bsz, nc, L, G, N)
    cm = xbc[..., SSD_INNER + G * N:].reshape(bsz, nc, L, G, N)
    dt = jax.nn.softplus(dt_raw.astype(jnp.float32) + dt_bias.astype(jnp.float32))
    dt = dt.reshape(bsz, nc, L, G, hpg)
    a = dt * (-jnp.exp(a_log.astype(jnp.float32))).reshape(G, hpg)
    a_cs = jnp.cumsum(a, axis=2)
    xdt = x * dt[..., None].astype(x.dtype)
    causal = jnp.tril(jnp.ones((L, L), dtype=bool))[None, None, :, :, None, None]
    seg = a_cs[:, :, :, None] - a_cs[:, :, None]
    decay = jnp.exp(jnp.where(causal, seg, -jnp.inf)).astype(x.dtype)
    cb = jnp.einsum('bclgn,bcsgn->bclsg', cm, bm)
    y_diag = jnp.einsum('bclsgh,bcsghp->bclghp', cb[..., None] * decay, xdt)
    decay_end = jnp.exp(a_cs[:, :, -1:] - a_cs).astype(x.dtype)
    states = jnp.einsum('bclgn,bclghp->bcghpn', bm, xdt * decay_end[..., None])
    chunk_decay = jnp.exp(a_cs[:, :, -1])

    def step(carry, inp):
        st, dec = inp
        return carry * dec[..., None, None] + st, carry

    init = jnp.zeros((bsz, G, hpg, P, N), jnp.float32)
    _, prev = lax.scan(step, init, (jnp.swapaxes(states.astype(jnp.float32), 0, 1),
                                    jnp.swapaxes(chunk_decay, 0, 1)))
    prev = jnp.swapaxes(prev, 0, 1).astype(x.dtype)
    y_off = jnp.einsum('bclgn,bcghpn->bclghp', cm, prev) * jnp.exp(a_cs)[..., None].astype(x.dtype)
    y = y_diag + y_off + x * d_skip.astype(x.dtype).reshape(G, hpg)[:, :, None]
    y = y.reshape(bsz, s, SSD_INNER)
    return rmsnorm(y * jax.nn.silu(z), norm_w)


def rel_bucket_band():
    W = WINDOW
    dist = np.arange(W)[:, None] - np.arange(2 * W)[None, :] + W
    n = np.maximum(dist, 0)
    max_exact = REL_BUCKETS // 2
    large = max_exact + (np.log(np.maximum(n, 1) / max_exact) / np.log(REL_MAX_DIST / max_exact)
                         * (REL_BUCKETS - max_exact)).astype(np.int32)
    large = np.minimum(large, REL_BUCKETS - 1)
    bucket = np.where(n < max_exact, n, large).astype(np.int32)
    valid = (dist >= 0) & (dist < W)
    return bucket, valid


def swa_attention(q, k, v, sinks, rel_table):
    bsz, s, _ = q.shape
    W, KV, Dh = WINDOW, ATT_KV_HEADS, ATT_HEAD_DIM
    G = ATT_Q_HEADS // KV
    nb = s // W
    q = q.reshape(bsz, nb, W, KV, G, Dh)
    k = k.reshape(bsz, nb, W, KV, Dh)
    v = v.reshape(bsz, nb, W, KV, Dh)
    pad = ((0, 0), (1, 0), (0, 0), (0, 0), (0, 0))
    kk = jnp.concatenate([jnp.pad(k, pad)[:, :-1], k], axis=2)
    vv = jnp.concatenate([jnp.pad(v, pad)[:, :-1], v], axis=2)
    scores = jnp.einsum('bnqhgd,bnkhd->bnhgqk', q, kk).astype(jnp.float32) / math.sqrt(Dh)
    bucket, valid = rel_bucket_band()
    bias = rel_table.astype(jnp.float32)[bucket]
    bias = jnp.transpose(bias, (2, 0, 1)).reshape(KV, G, W, 2 * W)
    first_ok = np.arange(2 * W) >= W
    mask = jnp.asarray(valid)[None] & ((jnp.arange(nb) > 0)[:, None, None] | jnp.asarray(first_ok)[None, None])
    scores = jnp.where(mask[None, :, None, None], scores + bias, -jnp.inf)
    sink = sinks.astype(jnp.float32).reshape(KV, G)[None, None, :, :, None, None]
    m = jnp.maximum(jnp.max(scores, axis=-1, keepdims=True), sink)
    p = jnp.exp(scores - m)
    p = p / (jnp.sum(p, axis=-1, keepdims=True) + jnp.exp(sink - m))
    out = jnp.einsum('bnhgqk,bnkhd->bnqhgd', p.astype(vv.dtype), vv)
    return out.reshape(bsz, s, ATT_WIDTH)


def chunk_sgu(u, gv, ln_w, ln_b, w_s, b_s):
    bsz, s, _ = u.shape
    C = GM_CHUNK
    nc = s // C
    u = jax.nn.gelu(u)
    gv = layernorm(jax.nn.gelu(gv), ln_w, ln_b).reshape(bsz, nc, C, GM_GROUPS, GM_GROUP_DIM)
    w = w_s * jnp.tril(jnp.ones((C, C), dtype=w_s.dtype))[None]
    mixed = jnp.einsum('gts,bcsgd->bctgd', w, gv) + jnp.transpose(b_s)[None, None, :, :, None]
    return u * mixed.reshape(bsz, s, GM_WIDTH)


def setup_inputs(seed: int = 0) -> dict:
    key = jax.random.key(seed)
    ks = jax.random.split(key, 32)

    def nrm(k, shape, scale):
        return jax.random.normal(k, shape, jnp.float32) * scale

    def gain(k, shape):
        return 1.0 + 0.02 * jax.random.normal(k, shape, jnp.float32)

    dt = jnp.exp(jax.random.uniform(ks[10], (DEPTH, SSD_HEADS), jnp.float32)
                 * (math.log(0.1) - math.log(0.001)) + math.log(0.001))
    return {
        "x": jax.random.normal(ks[0], (BATCH, SEQ, D_MODEL), jnp.float32),
        "ffn1_norm": gain(ks[1], (DEPTH, D_MODEL)),
        "ffn1_w_gate": nrm(ks[2], (DEPTH, D_MODEL, D_FF), D_MODEL ** -0.5),
        "ffn1_w_up": nrm(ks[3], (DEPTH, D_MODEL, D_FF), D_MODEL ** -0.5),
        "ffn1_w_down": nrm(ks[4], (DEPTH, D_FF, D_MODEL), D_FF ** -0.5),
        "mix_norm": gain(ks[5], (DEPTH, D_MODEL)),
        "w_in": nrm(ks[6], (DEPTH, D_MODEL, D_IN_PROJ), D_MODEL ** -0.5),
        "conv_w": nrm(ks[7], (DEPTH, SSD_CONV, SSD_XBC), SSD_CONV ** -0.5),
        "conv_b": nrm(ks[8], (DEPTH, SSD_XBC), 0.02),
        "dt_bias": dt + jnp.log(-jnp.expm1(-dt)),
        "a_log": jnp.log(jax.random.uniform(ks[11], (DEPTH, SSD_HEADS), jnp.float32, 1.0, 16.0)),
        "d_skip": gain(ks[12], (DEPTH, SSD_HEADS)),
        "ssd_norm": gain(ks[13], (DEPTH, SSD_INNER)),
        "attn_sinks": nrm(ks[14], (DEPTH, ATT_Q_HEADS), 0.5),
        "rel_bias": nrm(ks[15], (REL_BUCKETS, ATT_Q_HEADS), 0.5),
        "attn_out_norm": gain(ks[16], (DEPTH, ATT_WIDTH)),
        "sgu_ln_w": gain(ks[17], (DEPTH, GM_WIDTH)),
        "sgu_ln_b": nrm(ks[18], (DEPTH, GM_WIDTH), 0.02),
        "sgu_w": nrm(ks[19], (DEPTH, GM_GROUPS, GM_CHUNK, GM_CHUNK), GM_CHUNK ** -0.5),
        "sgu_b": 1.0 + nrm(ks[20], (DEPTH, GM_GROUPS, GM_CHUNK), 0.1),
        "sgu_out_norm": gain(ks[21], (DEPTH, GM_WIDTH)),
        "w_out": nrm(ks[22], (DEPTH, D_MIX, D_MODEL), D_MIX ** -0.5),
        "ffn2_norm": gain(ks[23], (DEPTH, D_MODEL)),
        "ffn2_w_gate": nrm(ks[24], (DEPTH, D_MODEL, D_FF), D_MODEL ** -0.5),
        "ffn2_w_up": nrm(ks[25], (DEPTH, D_MODEL, D_FF), D_MODEL ** -0.5),
        "ffn2_w_down": nrm(ks[26], (DEPTH, D_FF, D_MODEL), D_FF ** -0.5),
        "final_norm": gain(ks[27], (D_MODEL,)),
    }


def reference(x, ffn1_norm, ffn1_w_gate, ffn1_w_up, ffn1_w_down, mix_norm, w_in,
              conv_w, conv_b, dt_bias, a_log, d_skip, ssd_norm, attn_sinks, rel_bias,
              attn_out_norm, sgu_ln_w, sgu_ln_b, sgu_w, sgu_b, sgu_out_norm, w_out,
              ffn2_norm, ffn2_w_gate, ffn2_w_up, ffn2_w_down, final_norm):
    for l in range(DEPTH):
        x = x + 0.5 * swiglu(rmsnorm(x, ffn1_norm[l]), ffn1_w_gate[l], ffn1_w_up[l], ffn1_w_down[l])
        h = rmsnorm(x, mix_norm[l])
        proj = h @ w_in[l]
        y_ssd = ssd_mixer(proj[..., OFF_Z:OFF_XBC], proj[..., OFF_XBC:OFF_DT],
                          proj[..., OFF_DT:OFF_Q], conv_w[l], conv_b[l], dt_bias[l],
                          a_log[l], d_skip[l], ssd_norm[l])
        y_att = rmsnorm(swa_attention(proj[..., OFF_Q:OFF_K], proj[..., OFF_K:OFF_V],
                                      proj[..., OFF_V:OFF_U], attn_sinks[l], rel_bias),
                        attn_out_norm[l])
        y_sgu = rmsnorm(chunk_sgu(proj[..., OFF_U:OFF_GV], proj[..., OFF_GV:],
                                  sgu_ln_w[l], sgu_ln_b[l], sgu_w[l], sgu_b[l]),
                        sgu_out_norm[l])
        x = x + jnp.concatenate([y_ssd, y_att, y_sgu], axis=-1) @ w_out[l]
        x = x + 0.5 * swiglu(rmsnorm(x, ffn2_norm[l]), ffn2_w_gate[l], ffn2_w_up[l], ffn2_w_down[l])
    return rmsnorm(x, final_norm)
```

```python
import math
import numpy as np
import concourse.bass as bass
import concourse.mybir as mybir
from concourse.bass_utils import run_bass_kernel_spmd

F32 = mybir.dt.float32
BF16 = mybir.dt.bfloat16
AF = mybir.ActivationFunctionType
ALU = mybir.AluOpType

D = 1024
SEQ = 4096
DEPTH = 2
DFF = 2816
NJ = DFF // 128
DK = D // 128
T = 1024
NSUB = T // 512
OFF_Z, OFF_XBC, OFF_DT, OFF_Q, OFF_K, OFF_V, OFF_U, OFF_GV, DIN = 0, 512, 1536, 1544, 1800, 1928, 2056, 2312, 2568
EPS = 1e-6
NEG = -30000.0

SAME_ENGINE_SYNC = True


class View:
    __slots__ = ("ap", "regs")

    def __init__(self, ap, regs):
        self.ap = ap
        self.regs = regs


class Buf:
    def __init__(self, nc, name, shape, dtype, space="sbuf"):
        self.name = name
        self.shape = list(shape)
        self.dtype = dtype
        if space == "sbuf":
            self.t = nc.alloc_sbuf_tensor(name, list(shape), dtype)
        else:
            self.t = nc.alloc_psum_tensor(name, list(shape), dtype)
        fs = []
        s = 1
        for d in reversed(self.shape[1:]):
            fs.append(s)
            s *= d
        self.fstr = list(reversed(fs))

    def __getitem__(self, key):
        if not isinstance(key, tuple):
            key = (key,)
        key = key + (slice(None),) * (len(self.shape) - len(key))
        ap = self.t[key]
        pk = key[0]
        if isinstance(pk, slice):
            p0 = pk.start or 0
            p1 = pk.stop if pk.stop is not None else self.shape[0]
        else:
            p0, p1 = pk, pk + 1
        lo = 0
        hi = 0
        for k, st, d in zip(key[1:], self.fstr, self.shape[1:]):
            if isinstance(k, slice):
                a = k.start or 0
                b = k.stop if k.stop is not None else d
            else:
                a, b = k, k + 1
            lo += a * st
            hi += (b - 1) * st
        return View(ap, [(self.name, p0, p1, lo, hi + 1)])


def V(view, ap):
    return View(ap, view.regs)


class Op:
    __slots__ = ("eng", "fn", "reads", "writes", "dma_key", "dma_val", "idx", "waits", "signal", "sigval", "gid")


class Prog:
    ENGS = ("pe", "act", "dve", "pool", "sp")

    def __init__(self):
        self.ops = []
        self.per_eng = {e: [] for e in self.ENGS}
        self.dma_cnt = {}
        self.recs = {}
        self.seen = {e: {f: -1 for f in self.ENGS} for e in self.ENGS}
        self.seen_dma = {e: {} for e in self.ENGS}

    def add(self, eng, fn, reads=(), writes=(), dma=None):
        op = Op()
        op.eng = eng
        op.fn = fn
        def _bank(r):
            return (r[0], 0, 128, 0, 1 << 30) if r[0].startswith("ps") else r
        op.reads = [_bank(r) for v in reads for r in v.regs]
        op.writes = [_bank(r) for v in writes for r in v.regs]
        op.dma_key = dma
        op.dma_val = None
        if dma is not None:
            self.dma_cnt[dma] = self.dma_cnt.get(dma, 0) + 16
            op.dma_val = self.dma_cnt[dma]
        op.idx = len(self.per_eng[eng])
        op.gid = len(self.ops)
        op.waits = []
        op.signal = False
        op.sigval = None
        self._analyse(op)
        self.ops.append(op)
        self.per_eng[eng].append(op)
        return op

    @staticmethod
    def _ov(a, b):
        return a[1] < b[2] and b[1] < a[2] and a[3] < b[4] and b[3] < a[4]

    @staticmethod
    def _cov(a, b):
        return a[1] <= b[1] and a[2] >= b[2] and a[3] <= b[3] and a[4] >= b[4]

    def _analyse(self, op):
        deps = {}
        for r in op.reads:
            for rec in self.recs.get(r[0], ()):
                if rec[1] == "w" and self._ov(r, rec[0]):
                    deps[rec[2].gid] = rec[2]
        for w in op.writes:
            for rec in self.recs.get(w[0], ()):
                if self._ov(w, rec[0]):
                    deps[rec[2].gid] = rec[2]
        E = op.eng
        best = {}
        for p in deps.values():
            if p.dma_key is None:
                if p.eng not in best or best[p.eng].idx < p.idx:
                    best[p.eng] = p
        plist = [p for p in deps.values() if p.dma_key is not None] + list(best.values())
        for p in plist:
            if p.dma_key is not None:
                cur = self.seen_dma[E].get(p.dma_key, 0)
                if p.dma_val > cur:
                    self.seen_dma[E][p.dma_key] = p.dma_val
                    op.waits.append(("dma", p.dma_key, p.dma_val))
                continue
            F = p.eng
            if F == E and (E == "pe" or not SAME_ENGINE_SYNC):
                continue
            if self.seen[E][F] >= p.idx:
                continue
            self.seen[E][F] = p.idx
            p.signal = True
            op.waits.append(("eng", F, p))
        for w in op.writes:
            lst = self.recs.setdefault(w[0], [])
            lst[:] = [rec for rec in lst if not self._cov(w, rec[0])]
            lst.append((w, "w", op))
        for r in op.reads:
            lst = self.recs.setdefault(r[0], [])
            lst[:] = [rec for rec in lst if not (rec[1] == "r" and rec[2].eng == E and rec[2].dma_key is None
                                                 and op.dma_key is None and self._cov(r, rec[0]))]
            lst.append((r, "r", op))

    def emit(self, nc, tail_waits):
        cnt = {e: 0 for e in self.ENGS}
        for op in self.ops:
            if op.signal and op.dma_key is None:
                cnt[op.eng] += 1
                op.sigval = cnt[op.eng]
        for e in self.ENGS:
            assert cnt[e] < 60000, (e, cnt[e])
        esem = {e: nc.alloc_semaphore(name="sem_" + e) for e in self.ENGS}
        dsem = {k: nc.alloc_semaphore(name="dsem_" + k) for k in self.dma_cnt}
        self.stats = {e: (len(self.per_eng[e]), cnt[e]) for e in self.ENGS}
        prog = self

        def run(eng_name, e):
            for op in prog.per_eng[eng_name]:
                for w in op.waits:
                    if w[0] == "dma":
                        e.wait_ge(dsem[w[1]], w[2])
                    else:
                        e.wait_ge(esem[w[1]], w[2].sigval)
                ins = op.fn(e)
                if op.dma_key is not None:
                    ins.then_inc(dsem[op.dma_key], 16)
                elif op.signal:
                    ins.then_inc(esem[op.eng], 1)
            if eng_name == "sp":
                for k in tail_waits:
                    e.wait_ge(dsem[k], prog.dma_cnt[k])

        with nc.Block() as block:
            @block.tensor
            def _(e):
                run("pe", e)

            @block.scalar
            def _(e):
                run("act", e)

            @block.vector
            def _(e):
                run("dve", e)

            @block.gpsimd
            def _(e):
                run("pool", e)

            @block.sync
            def _(e):
                run("sp", e)


class SubBuf:
    def __init__(self, parent, off, shape, p0=0, p1=128):
        self.parent = parent
        self.off = off
        self.shape = [p1 - p0] + list(shape)
        self.p0 = p0
        size = 1
        for d in shape:
            size *= d
        self.size = size
        assert off + size <= parent.shape[1], (parent.name, off, size, parent.shape)
        base = parent.t[p0:p1, off:off + size]
        if len(shape) == 1:
            self.base = base
        elif len(shape) == 2:
            self.base = base.rearrange("p (a b) -> p a b", a=shape[0], b=shape[1])
        elif len(shape) == 3:
            self.base = base.rearrange("p (a b c) -> p a b c", a=shape[0], b=shape[1], c=shape[2])
        else:
            raise ValueError(shape)
        fs = []
        s = 1
        for d in reversed(self.shape[1:]):
            fs.append(s)
            s *= d
        self.fstr = list(reversed(fs))

    def __getitem__(self, key):
        if not isinstance(key, tuple):
            key = (key,)
        key = key + (slice(None),) * (len(self.shape) - len(key))
        ap = self.base[key]
        pk = key[0]
        if isinstance(pk, slice):
            p0 = pk.start or 0
            p1 = pk.stop if pk.stop is not None else self.shape[0]
        else:
            p0, p1 = pk, pk + 1
        lo = 0
        hi = 0
        for k, st, d in zip(key[1:], self.fstr, self.shape[1:]):
            if isinstance(k, slice):
                a = k.start or 0
                b = k.stop if k.stop is not None else d
            else:
                a, b = k, k + 1
            lo += a * st
            hi += (b - 1) * st
        return View(ap, [(self.parent.name, self.p0 + p0, self.p0 + p1, self.off + lo, self.off + hi + 1)])


def _aps(x):
    return x.ap if isinstance(x, View) else x


class Emit:
    def __init__(self, P):
        self.P = P

    def mm(self, out, lhsT, rhs, start=True, stop=True):
        o, l, r = out.ap, lhsT.ap, rhs.ap
        self.P.add("pe", lambda e: e.matmul(o, l, r, start=start, stop=stop), reads=[lhsT, rhs], writes=[out])

    def tr(self, out, in_, ident):
        o, i, d = out.ap, in_.ap, ident.ap
        self.P.add("pe", lambda e: e.transpose(o, i, d), reads=[in_, ident], writes=[out])

    def act(self, out, in_, func, bias=None, scale=None, accum=None, eng="act"):
        o, i = out.ap, in_.ap
        kw = {}
        reads = [in_]
        writes = [out]
        if bias is not None:
            kw["bias"] = _aps(bias)
            if isinstance(bias, View):
                reads.append(bias)
        if scale is not None:
            kw["scale"] = _aps(scale)
            if isinstance(scale, View):
                reads.append(scale)
        if accum is not None:
            kw["accum_out"] = accum.ap
            writes.append(accum)
        self.P.add(eng, lambda e: e.activation(o, i, func, **kw), reads=reads, writes=writes)

    def tt(self, out, in0, in1, op, eng="dve"):
        o, a, b = out.ap, in0.ap, in1.ap
        self.P.add(eng, lambda e: e.tensor_tensor(o, a, b, op), reads=[in0, in1], writes=[out])

    def ts(self, out, in0, s1, op0, s2=None, op1=None, eng="dve", accum=None):
        o, a = out.ap, in0.ap
        reads = [in0] + [s for s in (s1, s2) if isinstance(s, View)]
        writes = [out] + ([accum] if accum is not None else [])
        a1, a2 = _aps(s1), _aps(s2)
        kw = {}
        if accum is not None:
            kw["accum_out"] = accum.ap
        if op1 is None:
            self.P.add(eng, lambda e: e.tensor_scalar(o, a, a1, None, op0, **kw), reads=reads, writes=writes)
        else:
            self.P.add(eng, lambda e: e.tensor_scalar(o, a, a1, a2, op0, op1, **kw), reads=reads, writes=writes)

    def stt(self, out, in0, scalar, in1, op0, op1):
        o, a, b = out.ap, in0.ap, in1.ap
        s = _aps(scalar)
        reads = [in0, in1] + ([scalar] if isinstance(scalar, View) else [])
        self.P.add("dve", lambda e: e.scalar_tensor_tensor(o, a, s, b, op0, op1), reads=reads, writes=[out])

    def ttr(self, out, in0, in1, scale, scalar, op0, op1, accum):
        o, a, b, ac = out.ap, in0.ap, in1.ap, accum.ap
        s = _aps(scalar)
        reads = [in0, in1] + ([scalar] if isinstance(scalar, View) else [])
        self.P.add("dve", lambda e: e.tensor_tensor_reduce(o, a, b, scale, s, op0, op1, ac), reads=reads, writes=[out, accum])

    def copy(self, out, in_, eng="act"):
        o, i = out.ap, in_.ap
        if eng == "act":
            self.P.add("act", lambda e: e.copy(o, i), reads=[in_], writes=[out])
        else:
            self.P.add(eng, lambda e: e.tensor_copy(o, i), reads=[in_], writes=[out])

    def recip(self, out, in_):
        o, i = out.ap, in_.ap
        self.P.add("dve", lambda e: e.reciprocal(o, i), reads=[in_], writes=[out])

    def memset(self, out, val, eng="dve"):
        o = out.ap
        self.P.add(eng, lambda e: e.memset(o, val), writes=[out])

    def dma_in(self, eng, key, out, src_ap, **kw):
        o = out.ap
        self.P.add(eng, lambda e: e.dma_start(out=o, in_=src_ap, **kw), writes=[out], dma=key)

    def dma_out(self, eng, key, dst_ap, in_):
        i = in_.ap
        self.P.add(eng, lambda e: e.dma_start(out=dst_ap, in_=i), reads=[in_], dma=key)


CST_IDENT, CST_TRI, CST_TRIL, CST_ESEL, CST_INVN, CST_W = 0, 128, 256, 384, 640, 648
PFM_L = 72
PFM_W = 2 * PFM_L + 8
PROW_L = 540
PROW_W = 2 * PROW_L


def host_consts():
    c = np.zeros((128, CST_W), np.float32)
    c[:, CST_IDENT:CST_IDENT + 128] = np.eye(128, dtype=np.float32)
    s = np.arange(128)
    c[:, CST_TRI:CST_TRI + 128] = (s[:, None] <= s[None, :]).astype(np.float32)
    c[:, CST_TRIL:CST_TRIL + 128] = (s[None, :] <= s[:, None]).astype(np.float32)
    for g in range(4):
        c[g, CST_ESEL + g * 64:CST_ESEL + (g + 1) * 64] = 1.0
    c[:, CST_INVN + 0] = 1.0 / 512
    c[:, CST_INVN + 1] = 1.0 / 256
    c[:, CST_INVN + 2] = 1.0 / 256
    return c


def rel_bucket_band():
    W = 128
    dist = np.arange(W)[:, None] - np.arange(2 * W)[None, :] + W
    n = np.maximum(dist, 0)
    max_exact = 16
    large = max_exact + (np.log(np.maximum(n, 1) / max_exact) / np.log(128 / max_exact) * (32 - max_exact)).astype(np.int32)
    large = np.minimum(large, 31)
    bucket = np.where(n < max_exact, n, large).astype(np.int32)
    valid = (dist >= 0) & (dist < W)
    return bucket, valid


def build(nt=4, nlayers=2, parts=("ffn1", "mix", "ffn2"), final_norm=True, dbg=None):
    nc = bass.Bass("TRN2", target_bir_lowering=False)
    P = Prog()
    E = Emit(P)

    def dram(name, shape, kind="ExternalInput"):
        return nc.dram_tensor(name, list(shape), F32, kind=kind).ap()

    xin = dram("x", [SEQ, D])
    out = dram("out", [SEQ, D], kind="ExternalOutput")
    w_gate = [dram("ffn1_w_gate", [DEPTH, D, DFF]), dram("ffn2_w_gate", [DEPTH, D, DFF])]
    w_up = [dram("ffn1_w_up", [DEPTH, D, DFF]), dram("ffn2_w_up", [DEPTH, D, DFF])]
    w_down = [dram("ffn1_w_down", [DEPTH, DFF, D]), dram("ffn2_w_down", [DEPTH, DFF, D])]
    w_in = dram("w_in", [DEPTH, D, DIN])
    w_out = dram("w_out", [DEPTH, D, D])
    d_cst = dram("cst", [128, CST_W])
    d_pfm = dram("pfm", [128, PFM_W])
    d_prow = dram("prow", [1, PROW_W])
    d_crow = dram("crow", [1, 2 * 1024])
    d_sgub = dram("sgub", [4, 2, 128])
    d_sguw = dram("sguw", [DEPTH, 4, 128, 128])
    d_t5b = dram("t5b", [128, 4, 256])
    d_dbg = dram("dbg", [128, 1024], kind="ExternalOutput") if dbg else None

    x = Buf(nc, "xres", [128, DK, T], F32)
    xn = Buf(nc, "xn", [128, DK, T], BF16)
    RH = Buf(nc, "RH", [128, 16448], BF16)
    RB = Buf(nc, "RB", [128, DK * DIN], BF16)
    RD = Buf(nc, "RD", [128, 11264], BF16)
    SF = Buf(nc, "SF", [128, 5120], F32)
    SB = Buf(nc, "SB", [128, 5632], BF16)

    h = SubBuf(RH, 0, [11, T])
    sq = SubBuf(RH, 0, [DK, 512])
    xbc = SubBuf(RH, 0, [8, 516])
    bcf = SubBuf(RH, 4128, [4, 512])
    ymf = SubBuf(RH, 6176, [8, 512])
    wkd = SubBuf(RH, 10272, [DK, 256])
    diag = SubBuf(RH, 12320, [32, 128])

    wgs = [SubBuf(RB, i * 4096, [DK, 512]) for i in range(2)]
    wus = [SubBuf(RB, 8192 + i * 4096, [DK, 512]) for i in range(2)]
    win = SubBuf(RB, 0, [DK, DIN])

    wds = [SubBuf(RD, i * 2816, [11, 256]) for i in range(4)]
    wout = SubBuf(RD, 0, [DK, D])
    qf = SubBuf(RD, 8192, [2, 512])
    kkf = SubBuf(RD, 9216, [2, 640])
    vall = SubBuf(RD, 10496, [5, 128])

    rs = SubBuf(SF, 0, [512])
    rstd = SubBuf(SF, 512, [512])
    sgs = [SubBuf(SF, 1024 + i * 512, [512]) for i in range(2)]
    xtm_in = SubBuf(SF, 2048, [1024])
    otm = [SubBuf(SF, 3072 + i * 1024, [1024]) for i in range(2)]
    xls = [xtm_in, otm[0], otm[1]]
    Rp = SubBuf(SF, 2048, [8, 128])
    ssb = SubBuf(SF, 2048, [4, 256])
    szb = SubBuf(SF, 3072, [512])
    glb = SubBuf(SF, 3072, [512])
    t1b = SubBuf(SF, 3584, [512])
    gtmp = SubBuf(SF, 3584, [256])
    sgo = SubBuf(SF, 3840, [256])
    yzb = SubBuf(SF, 4096, [512])
    attb = SubBuf(SF, 4608, [256])
    sm = SubBuf(SF, 4864, [256])
    xtm = SubBuf(SB, 0, [512])
    btm = SubBuf(SB, 512, [256])
    xdt = SubBuf(SB, 768, [512])
    xdtd = SubBuf(SB, 1280, [512])
    decT = SubBuf(SB, 1792, [8, 128])
    pb = SubBuf(SB, 1792, [4, 256])
    Mb = SubBuf(SB, 2816, [8, 128])
    pTb = SubBuf(SB, 2816, [4, 2, 128])
    stbf = SubBuf(SB, 3840, [512])
    gvn = SubBuf(SB, 4352, [256])
    ymt = SubBuf(SB, 4608, [1024])

    cst = Buf(nc, "cstb", [128, CST_W], F32)
    ident_f = cst[:, CST_IDENT:CST_IDENT + 128]
    tri_f = cst[:, CST_TRI:CST_TRI + 128]
    tril_f = cst[:, CST_TRIL:CST_TRIL + 128]
    esel_f = cst[0:4, CST_ESEL:CST_ESEL + 256]
    invn3 = cst[:, CST_INVN:CST_INVN + 3]
    ones_f = Buf(nc, "ones_f", [128, 128], F32)
    cb16 = Buf(nc, "cb16", [128, 3, 128], BF16)
    ident_b = cb16[:, 0, :]
    ones_b = cb16[:, 1, :]
    negm = Buf(nc, "negm", [128, 4, 128], BF16)
    pfm = Buf(nc, "pfmb", [128, PFM_W], F32)
    prow = Buf(nc, "prowb", [128, PROW_W], F32)
    nega = Buf(nc, "nega", [128, 2, 8], F32)
    crow_f = Buf(nc, "crow_f", [1, 2048], F32)
    crow = Buf(nc, "crowb", [1, 2048], BF16)
    sgub = Buf(nc, "sgubb", [4, 2, 128], F32)
    wsT = Buf(nc, "wsT", [128, 2, 4, 128], BF16)
    t5b = Buf(nc, "t5bb", [128, 4, 256], F32)
    state = Buf(nc, "state", [128, 2, 512], F32)
    halo = Buf(nc, "halo", [128, 2, 8, 3], BF16)
    kprev = Buf(nc, "kprev", [128, 2, 2, 128], BF16)
    vprev = Buf(nc, "vprev", [128, 2, 128], BF16)

    ps = [Buf(nc, "ps%d" % i, [128, 512], F32, space="psum") for i in range(7)]
    psT = Buf(nc, "psT", [128, 1024], BF16, space="psum")

    E.dma_in("sp", "c0", cst[:, :], d_cst)
    E.dma_in("sp", "c1", pfm[:, :], d_pfm)
    E.dma_in("sp", "c2", prow[:, :], d_prow[0:1, :].broadcast_to([128, PROW_W]))
    E.dma_in("sp", "c3", crow_f[:, :], d_crow)
    E.dma_in("sp", "c4", sgub[:, :, :], d_sgub)
    E.dma_in("sp", "c5", t5b[:, :, :], d_t5b)
    E.memset(ones_f[:, :], 1.0)
    E.copy(ident_b, ident_f, eng="dve")
    E.memset(ones_b, 1.0)
    E.copy(crow[:, :], crow_f[:, :], eng="dve")
    for r in range(4):
        E.ts(negm[:, r, :], tri_f, -1.0, ALU.add, -NEG, ALU.mult)
    E.memset(state[:, :, :], 0.0)
    E.memset(halo[:, :, :, :], 0.0)
    E.memset(kprev[:, :, :, :], 0.0)
    E.memset(vprev[:, :, :], 0.0)
    for l in range(nlayers):
        E.act(nega[:, l, :], prow[:, l * PROW_L + 8:l * PROW_L + 16], AF.Exp)
        E.ts(nega[:, l, :], nega[:, l, :], -1.0, ALU.mult)
        wn = SubBuf(SF, 0, [4, 128])
        E.dma_in("sp", "c6", wn[:, :, :], d_sguw[l].rearrange("g t s -> t g s"))
        for g in range(4):
            E.tt(wn[:, g, :], wn[:, g, :], tril_f, ALU.mult)
        for g in range(4):
            E.tr(ps[0][:, g * 128:(g + 1) * 128], wn[:, g, :], ident_f)
        E.copy(wsT[:, l, :, :], V(ps[0][:, :], ps[0].t[:, :].rearrange("p (g t) -> p g t", g=4)))

    def pcol(l, off, n=1):
        return pfm[:, l * PFM_L + off:l * PFM_L + off + n]

    def rmsnorm_fm(wbase, to_x=False):
        for n in range(NSUB):
            ns = slice(n * 512, (n + 1) * 512)
            E.act(sq[:, :, :], x[:, :, ns], AF.Square)
            for dk in range(DK):
                E.mm(ps[6][:, :], ones_b, sq[:, dk, :], start=(dk == 0), stop=(dk == DK - 1))
            E.act(rs[:, :], ps[6][:, :], AF.Sqrt, bias=pfm_eps, scale=1.0 / D)
            E.recip(rstd[:, :], rs[:, :])
            for dk in range(DK):
                dst = x[:, dk, ns] if to_x else xn[:, dk, ns]
                E.stt(dst, x[:, dk, ns], pfm[:, wbase + dk:wbase + dk + 1], rstd[:, :], ALU.mult, ALU.mult)

    cnt = {"gu": 0, "y": 0, "wg": 0, "wd": 0}

    def ffn(l, which):
        f = 0 if which == "ffn1" else 1
        rmsnorm_fm(l * PFM_L + (0 if f == 0 else 16))
        wgv = w_gate[f][l].rearrange("(k p) c -> p k c", p=128)
        wuv = w_up[f][l].rearrange("(k p) c -> p k c", p=128)
        wdv = w_down[f][l].rearrange("(j p) c -> p j c", p=128)
        for hh in range(2):
            j0 = hh * 11
            for (ga, gb) in ((0, 4), (4, 8), (8, 11)):
                w = (gb - ga) * 128
                slot = cnt["wg"] % 2
                cnt["wg"] += 1
                c0 = (j0 + ga) * 128
                E.dma_in("pool", "wg%d" % slot, wgs[slot][:, :, 0:w], wgv[:, :, c0:c0 + w])
                E.dma_in("pool", "wu%d" % slot, wus[slot][:, :, 0:w], wuv[:, :, c0:c0 + w])
                for jj in range(ga, gb):
                    jc = (jj - ga) * 128
                    for n in range(NSUB):
                        ns = slice(n * 512, (n + 1) * 512)
                        k = cnt["gu"] % 2
                        cnt["gu"] += 1
                        gps, ups = ps[k], ps[2 + k]
                        for dk in range(DK):
                            E.mm(gps[:, :], wgs[slot][:, dk, jc:jc + 128], xn[:, dk, ns], start=(dk == 0), stop=(dk == DK - 1))
                        for dk in range(DK):
                            E.mm(ups[:, :], wus[slot][:, dk, jc:jc + 128], xn[:, dk, ns], start=(dk == 0), stop=(dk == DK - 1))
                        E.act(sgs[k][:, :], gps[:, :], AF.Silu)
                        E.tt(h[:, jj, ns], sgs[k][:, :], ups[:, :], ALU.mult)
            for dp in range(4):
                slot = cnt["wd"] % 4
                cnt["wd"] += 1
                E.dma_in("pool", "wd%d" % slot, wds[slot][:, :, :], wdv[:, j0:j0 + 11, dp * 256:(dp + 1) * 256])
                for n in range(NSUB):
                    ns = slice(n * 512, (n + 1) * 512)
                    for dd in range(2):
                        dc = dp * 2 + dd
                        yps = ps[4 + cnt["y"] % 2]
                        cnt["y"] += 1
                        for jj in range(11):
                            E.mm(yps[:, :], wds[slot][:, jj, dd * 128:(dd + 1) * 128], h[:, jj, ns], start=(jj == 0), stop=(jj == 10))
                        E.stt(x[:, dc, ns], yps[:, :], 0.5, x[:, dc, ns], ALU.mult, ALU.add)

    def load_x(ti):
        for c in range(T // 128):
            t0 = ti * T + c * 128
            xb = xls[c % 3]
            E.dma_in("sp", "xl%d" % (c % 3), xb[:, :], xin[t0:t0 + 128, :])
            for b in range(2):
                pb_ = ps[(2 * c + b) % 4]
                for q in range(4):
                    dk = b * 4 + q
                    E.tr(pb_[:, q * 128:(q + 1) * 128], xb[:, dk * 128:(dk + 1) * 128], ident_f)
                E.copy(x[:, b * 4:(b + 1) * 4, c * 128:(c + 1) * 128],
                       V(pb_[:, :], pb_.t[:, :].rearrange("p (q t) -> p q t", q=4)), eng=("act" if b == 0 else "dve"))

    def store_x(ti):
        for c in range(T // 128):
            t0 = ti * T + c * 128
            ob = otm[c % 2]
            for b in range(2):
                pb_ = ps[4 + b]
                for q in range(4):
                    dk = b * 4 + q
                    E.tr(pb_[:, q * 128:(q + 1) * 128], x[:, dk, c * 128:(c + 1) * 128], ident_f)
                E.copy(ob[:, b * 512:(b + 1) * 512], pb_[:, :], eng=("act" if b == 0 else "dve"))
            E.dma_out("sp", "xs%d" % (c % 2), out[t0:t0 + 128, :], ob[:, :])

    pfm_eps = EPS

    GELU = AF.Gelu_apprx_tanh

    def bc(v, n0, n1):
        return V(v, v.ap.unsqueeze(2).broadcast_to([128, n0, n1]))

    def v3(sub, a):
        return V(sub[:, :], sub.base.rearrange("p (a b) -> p a b", a=a))

    RpF = [SubBuf(SF, 2048 + i * 512, [512]) for i in range(2)]
    negmF = V(negm[:, :, :], negm.t[:, :, :].rearrange("p a b -> p (a b)"))

    def mixer(l, ti):
        pb_ = l * PFM_L
        rb = l * PROW_L
        for dk in range(DK):
            E.dma_in("pool", "win%d" % dk, win[:, dk, :], w_in[l][dk * 128:(dk + 1) * 128, :], max_dma_last_dim=4096)
        E.dma_in("pool", "wout", wout[:, :, :], w_out[l].rearrange("(k p) c -> p k c", p=128), max_dma_last_dim=4096)
        wv = w_in[l].rearrange("(k p) c -> p k c", p=128)
        for g in range(2):
            for r in range(2):
                i = g * 2 + r
                E.dma_in("pool", "wkd%d" % i, wkd[:, :, i * 64:(i + 1) * 64], wv[:, :, OFF_K + g * 64:OFF_K + (g + 1) * 64])
        rmsnorm_fm(pb_ + 8)
        for fc in range(8):
            for k in range(4):
                E.ts(diag[:, fc * 4 + k, :], ident_f, pfm[:, pb_ + 24 + fc * 4 + k:pb_ + 25 + fc * 4 + k], ALU.mult)
        E.copy(stbf[:, :], state[:, l, :])

        def smv(o, n):
            return sm[:, o:o + n]
        dtr, ex, dt, a_, acs, nacs = smv(0, 8), smv(8, 8), smv(16, 8), smv(24, 8), smv(32, 8), smv(40, 8)
        eacs, dd, dend, cd, w2 = smv(48, 8), smv(56, 8), smv(64, 8), smv(72, 8), smv(80, 8)
        ss3, ms3, ri3 = smv(88, 3), smv(92, 3), smv(96, 3)
        m4, nm4, sum4, es4, den4, rden4 = smv(100, 4), smv(104, 4), smv(108, 4), smv(112, 4), smv(116, 4), smv(120, 4)
        st6, mv, rsl = smv(128, 6), smv(136, 2), smv(140, 1)
        dskip = prow[:, rb + 16:rb + 24]
        sinks = prow[:, rb + 24:rb + 28]
        lnw = prow[:, rb + 28:rb + 284]
        lnb = prow[:, rb + 284:rb + 540]

        for hf in range(2):
            ns = slice(hf * 512, (hf + 1) * 512)
            E.copy(xbc[:, :, 0:3], halo[:, l, :, :], eng="dve")
            E.copy(kkf[:, :, 0:128], kprev[:, l, :, :], eng="dve")
            E.copy(vall[:, 0, :], vprev[:, l, :], eng="dve")
            k = 0
            for fc in range(8):
                pt = ps[k % 4]
                k += 1
                for dk in range(DK):
                    E.mm(pt[:, :], win[:, dk, OFF_XBC + fc * 128:OFF_XBC + (fc + 1) * 128], xn[:, dk, ns], start=(dk == 0), stop=(dk == DK - 1))
                E.copy(xbc[:, fc, 3:515], pt[:, :])
            for qc in range(2):
                pt = ps[k % 4]
                k += 1
                for dk in range(DK):
                    E.mm(pt[:, :], win[:, dk, OFF_Q + qc * 128:OFF_Q + (qc + 1) * 128], xn[:, dk, ns], start=(dk == 0), stop=(dk == DK - 1))
                E.act(qf[:, qc, :], pt[:, :], AF.Identity, scale=0.125)
            for g in range(2):
                pt = ps[k % 4]
                k += 1
                for dk in range(DK):
                    E.mm(pt[:, :], wkd[:, dk, g * 128:(g + 1) * 128], xn[:, dk, ns], start=(dk == 0), stop=(dk == DK - 1))
                E.copy(kkf[:, g, 128:640], pt[:, :])
            for fc in range(4, 8):
                pt = ps[k % 4]
                k += 1
                for kk_ in range(4):
                    E.mm(pt[:, :], diag[:, fc * 4 + kk_, :], xbc[:, fc, kk_:kk_ + 512], start=(kk_ == 0), stop=(kk_ == 3))
                E.act(bcf[:, fc - 4, :], pt[:, :], AF.Silu, bias=pfm[:, pb_ + 56 + fc:pb_ + 57 + fc])

            for c in range(4):
                l0 = c * 128
                first = (ti == 0 and hf == 0 and c == 0)
                tok = slice(hf * 512 + l0, hf * 512 + l0 + 128)
                z_ps, uv_ps, vd_ps = ps[4], ps[5], ps[6]
                for dk in range(DK):
                    E.mm(z_ps[:, :], xn[:, dk, tok], win[:, dk, OFF_Z:OFF_Z + 512], start=(dk == 0), stop=(dk == DK - 1))
                for dk in range(DK):
                    E.mm(uv_ps[:, :], xn[:, dk, tok], win[:, dk, OFF_U:OFF_U + 512], start=(dk == 0), stop=(dk == DK - 1))
                for dk in range(DK):
                    E.mm(vd_ps[:, 0:128], xn[:, dk, tok], win[:, dk, OFF_V:OFF_V + 128], start=(dk == 0), stop=(dk == DK - 1))
                for dk in range(DK):
                    E.mm(vd_ps[:, 128:136], xn[:, dk, tok], win[:, dk, OFF_DT:OFF_DT + 8], start=(dk == 0), stop=(dk == DK - 1))
                xc_ps, bt_ps = ps[0], ps[1]
                for fc in range(4):
                    o = fc * 128
                    for kk_ in range(4):
                        E.mm(xc_ps[:, o:o + 128], xbc[:, fc, l0 + kk_:l0 + kk_ + 128], diag[:, fc * 4 + kk_, :], start=(kk_ == 0), stop=False)
                    E.mm(xc_ps[:, o:o + 128], ones_b[0:1, :] if False else cb16[0:1, 1, :], crow[0:1, l * 1024 + fc * 128:l * 1024 + (fc + 1) * 128], start=False, stop=True)
                for fc in (4, 5):
                    o = (fc - 4) * 128
                    for kk_ in range(4):
                        E.mm(bt_ps[:, o:o + 128], xbc[:, fc, l0 + kk_:l0 + kk_ + 128], diag[:, fc * 4 + kk_, :], start=(kk_ == 0), stop=False)
                    E.mm(bt_ps[:, o:o + 128], cb16[0:1, 1, :], crow[0:1, l * 1024 + fc * 128:l * 1024 + (fc + 1) * 128], start=False, stop=True)
                E.act(xtm[:, :], xc_ps[:, :], AF.Silu)
                E.act(btm[:, :], bt_ps[:, 0:256], AF.Silu)
                E.copy(vall[:, c + 1, :], vd_ps[:, 0:128])
                E.tt(dtr, vd_ps[:, 128:136], prow[:, rb:rb + 8], ALU.add)
                E.act(ex, dtr, AF.Exp)
                E.act(dt, ex, AF.Ln, bias=1.0)
                E.tt(a_, dt, nega[:, l, :], ALU.mult)
                acs_ps, tot_ps = vd_ps[:, 144:152], vd_ps[:, 152:160]
                E.mm(acs_ps, tri_f, a_, start=True, stop=True)
                E.mm(tot_ps, ones_f[:, :], a_, start=True, stop=True)
                E.copy(acs, acs_ps, eng="dve")
                E.ts(nacs, acs_ps, -1.0, ALU.mult)
                E.act(eacs, acs_ps, AF.Exp)
                E.tt(dd, tot_ps, acs, ALU.subtract)
                E.act(dend, dd, AF.Exp)
                E.act(cd, tot_ps, AF.Exp)
                E.tt(w2, dt, dend, ALU.mult)
                x3 = v3(xtm, 8)
                E.tt(v3(xdt, 8), x3, bc(dt, 8, 64), ALU.mult)
                E.tt(v3(xdtd, 8), x3, bc(w2, 8, 64), ALU.mult)
                E.tt(Rp[:, :, :], V(ident_f, ident_f.ap.unsqueeze(1).broadcast_to([128, 8, 128])), bc(acs, 8, 128), ALU.mult)
                for hb in range(2):
                    sp_ = ps[2 + hb]
                    E.mm(sp_[:, :], ones_f[:, :], RpF[hb][:, :], start=True, stop=False)
                    E.mm(sp_[:, :], ident_b, negmF, start=False, stop=True)
                    for hh_ in range(4):
                        hd = hb * 4 + hh_
                        E.act(decT[:, hd, :], sp_[:, hh_ * 128:(hh_ + 1) * 128], AF.Exp, bias=sm[:, 40 + hd:41 + hd])
                for g in range(2):
                    E.mm(bt_ps[:, 256 + g * 128:256 + (g + 1) * 128], bcf[:, g, l0:l0 + 128], bcf[:, 2 + g, l0:l0 + 128], start=True, stop=True)
                for g in range(2):
                    cbv = bt_ps[:, 256 + g * 128:256 + (g + 1) * 128]
                    E.tt(Mb[:, g * 4:(g + 1) * 4, :], decT[:, g * 4:(g + 1) * 4, :],
                         V(cbv, cbv.ap.unsqueeze(1).broadcast_to([128, 4, 128])), ALU.mult)
                y_ps, yo_ps, st_ps = ps[0], ps[2], ps[3]
                for hd in range(8):
                    E.mm(y_ps[:, hd * 64:(hd + 1) * 64], Mb[:, hd, :], xdt[:, hd * 64:(hd + 1) * 64], start=True, stop=True)
                for g in range(2):
                    E.mm(yo_ps[:, g * 256:(g + 1) * 256], bcf[:, 2 + g, l0:l0 + 128], stbf[:, g * 256:(g + 1) * 256], start=True, stop=True)
                for g in range(2):
                    E.mm(st_ps[:, g * 256:(g + 1) * 256], btm[:, g * 128:(g + 1) * 128], xdtd[:, g * 256:(g + 1) * 256], start=True, stop=True)
                st3 = V(state[:, l, :], state.t[:, l, :].rearrange("p (h d) -> p h d", h=8))
                E.tt(st3, st3, bc(cd, 8, 64), ALU.mult)
                E.tt(state[:, l, :], state[:, l, :], st_ps[:, :], ALU.add)
                E.copy(stbf[:, :], state[:, l, :])
                yo3 = V(yo_ps[:, :], yo_ps.t[:, :].rearrange("p (h d) -> p h d", h=8))
                E.tt(v3(t1b, 8), yo3, bc(eacs, 8, 64), ALU.mult)
                E.tt(t1b[:, :], t1b[:, :], y_ps[:, :], ALU.add)
                E.tt(v3(yzb, 8), x3, bc(dskip, 8, 64), ALU.mult)
                E.tt(yzb[:, :], yzb[:, :], t1b[:, :], ALU.add)
                E.act(szb[:, :], z_ps[:, :], AF.Silu)
                E.tt(yzb[:, :], yzb[:, :], szb[:, :], ALU.mult)
                E.act(szb[:, :], yzb[:, :], AF.Square, accum=sm[:, 88:89])
                kbs = [1] if first else [0, 1]
                nk = 128 * len(kbs)
                kc0 = l0 + (128 if first else 0)
                bc0 = 128 if first else 0
                for hq in range(4):
                    kvh = hq // 2
                    pbse = (hq % 2) * 64
                    s_ps = ps[1 + hq // 2][:, (hq % 2) * 256:(hq % 2) * 256 + nk]
                    E.mm(s_ps, qf[pbse:pbse + 64, hq // 2, l0:l0 + 128], kkf[pbse:pbse + 64, kvh, kc0:kc0 + nk], start=True, stop=True)
                    E.tt(ssb[:, hq, 0:nk], s_ps, t5b[:, hq, bc0:bc0 + nk], ALU.add)
                E.P.add("dve", (lambda o, i: (lambda e: e.tensor_reduce(o, i, mybir.AxisListType.X, ALU.max)))(m4.ap, ssb[:, :, 0:nk].ap),
                        reads=[ssb[:, :, 0:nk]], writes=[m4])
                E.tt(m4, m4, sinks, ALU.max)
                E.ts(nm4, m4, -1.0, ALU.mult)
                for hq in range(4):
                    E.act(pb[:, hq, 0:nk], ssb[:, hq, 0:nk], AF.Exp, bias=sm[:, 104 + hq:105 + hq], accum=sm[:, 108 + hq:109 + hq])
                E.tt(es4, sinks, nm4, ALU.add)
                E.act(es4, es4, AF.Exp)
                E.tt(den4, sum4, es4, ALU.add)
                E.recip(rden4, den4)
                for hq in range(4):
                    for i in range(len(kbs)):
                        E.tr(psT[:, (hq * 2 + i) * 128:(hq * 2 + i + 1) * 128], pb[:, hq, i * 128:(i + 1) * 128], ident_b)
                psT4 = psT.t[:, :].rearrange("p (h k q) -> p h k q", h=4, k=2)
                if first:
                    E.copy(V(pTb[:, :, 0, :], pTb.base[:, :, 0, :]), V(psT[:, :], psT4[:, :, 0, :]))
                else:
                    E.copy(pTb[:, :, :, :], V(psT[:, :], psT4))
                for hq in range(4):
                    kvh = hq // 2
                    for i, kb in enumerate(kbs):
                        E.mm(ps[3][:, hq * 64:(hq + 1) * 64], pTb[:, hq, i, :], vall[:, c + kb, kvh * 64:(kvh + 1) * 64],
                             start=(i == 0), stop=(i == len(kbs) - 1))
                o3 = V(ps[3][:, 0:256], ps[3].t[:, 0:256].rearrange("p (h d) -> p h d", h=4))
                E.tt(v3(attb, 4), o3, bc(rden4, 4, 64), ALU.mult)
                E.act(szb[:, 0:256], attb[:, :], AF.Square, accum=sm[:, 89:90])
                gelu(glb, uv_ps)
                E.P.add("dve", (lambda o, i: (lambda e: e.tensor_reduce(o, i, mybir.AxisListType.X, ALU.add)))(sm[:, 136:137].ap, glb[:, 256:512].ap),
                        reads=[glb[:, 256:512]], writes=[sm[:, 136:137]])
                E.ts(sm[:, 136:137], sm[:, 136:137], 1.0 / 256, ALU.mult)
                E.ts(gtmp[:, :], glb[:, 256:512], sm[:, 136:137], ALU.subtract)
                E.act(sgo[:, :], gtmp[:, :], AF.Square, accum=sm[:, 137:138])
                E.ts(rsl, sm[:, 137:138], 1.0 / 256, ALU.mult, EPS, ALU.add)
                E.act(rsl, rsl, AF.Sqrt)
                E.recip(rsl, rsl)
                E.ts(gtmp[:, :], gtmp[:, :], rsl, ALU.mult)
                E.tt(gtmp[:, :], gtmp[:, :], lnw, ALU.mult)
                E.tt(gvn[:, :], gtmp[:, :], lnb, ALU.add)
                mx_ps = ps[3][:, 256:512]
                E.mm(mx_ps, sgub[0:4, l, :], esel_f, start=True, stop=False)
                for g in range(4):
                    E.mm(ps[3][:, 256 + g * 64:256 + (g + 1) * 64], wsT[:, l, g, :], gvn[:, g * 64:(g + 1) * 64], start=False, stop=(g == 3))
                E.tt(sgo[:, :], mx_ps, glb[:, 0:256], ALU.mult)
                E.act(szb[:, 256:512], sgo[:, :], AF.Square, accum=sm[:, 90:91])
                E.tt(ms3, ss3, invn3, ALU.mult)
                E.act(ms3, ms3, AF.Sqrt, bias=pfm_eps)
                E.recip(ri3, ms3)
                E.ts(ymt[:, 0:512], yzb[:, :], sm[:, 96:97], ALU.mult)
                E.ts(ymt[:, 512:768], attb[:, :], sm[:, 97:98], ALU.mult)
                E.ts(ymt[:, 768:1024], sgo[:, :], sm[:, 98:99], ALU.mult)
                for kc in range(8):
                    E.tr(psT[:, kc * 128:(kc + 1) * 128], ymt[:, kc * 128:(kc + 1) * 128], ident_b)
                for kc in range(8):
                    E.act(ymf[:, kc, l0:l0 + 128], psT[:, kc * 128:(kc + 1) * 128], AF.Identity, scale=pfm[:, pb_ + 64 + kc:pb_ + 65 + kc])
            for dc in range(8):
                pt = ps[dc % 4]
                for kc in range(8):
                    E.mm(pt[:, :], wout[:, kc, dc * 128:(dc + 1) * 128], ymf[:, kc, :], start=(kc == 0), stop=(kc == 7))
                E.tt(x[:, dc, ns], x[:, dc, ns], pt[:, :], ALU.add)
            E.copy(halo[:, l, :, :], xbc[:, :, 512:515], eng="dve")
            E.copy(kprev[:, l, :, :], kkf[:, :, 512:640], eng="dve")
            E.copy(vprev[:, l, :], vall[:, 4, :], eng="dve")

    def gelu(dst, src_ps):
        if GELU is not None:
            E.act(dst[:, :], src_ps[:, :], GELU)
        else:
            tmp = SubBuf(SF, 4096, [512])
            raise NotImplementedError


    for ti in range(nt):
        load_x(ti)
        for l in range(nlayers):
            for part in parts:
                if part in ("ffn1", "ffn2"):
                    ffn(l, part)
                elif part == "mix":
                    mixer(l, ti)
        if final_norm:
            rmsnorm_fm(2 * PFM_L, to_x=True)
        store_x(ti)

    if dbg:
        pass
    P.emit(nc, tail_waits=["xs0", "xs1"])
    return nc, P


def _prep(inputs):
    g = {k: np.asarray(v) for k, v in inputs.items()}
    pfm = np.zeros((128, PFM_W), np.float32)
    prow = np.zeros((1, PROW_W), np.float32)
    for l in range(DEPTH):
        b = l * PFM_L
        pfm[:, b + 0:b + 8] = g["ffn1_norm"][l].reshape(8, 128).T
        pfm[:, b + 8:b + 16] = g["mix_norm"][l].reshape(8, 128).T
        pfm[:, b + 16:b + 24] = g["ffn2_norm"][l].reshape(8, 128).T
        pfm[:, b + 24:b + 56] = g["conv_w"][l].reshape(4, 8, 128).transpose(2, 1, 0).reshape(128, 32)
        pfm[:, b + 56:b + 64] = g["conv_b"][l].reshape(8, 128).T
        mixw = np.concatenate([g["ssd_norm"][l], g["attn_out_norm"][l], g["sgu_out_norm"][l]])
        pfm[:, b + 64:b + 72] = mixw.reshape(8, 128).T
        r = l * PROW_L
        prow[0, r + 0:r + 8] = g["dt_bias"][l]
        prow[0, r + 8:r + 16] = g["a_log"][l]
        prow[0, r + 16:r + 24] = g["d_skip"][l]
        prow[0, r + 24:r + 28] = g["attn_sinks"][l]
        prow[0, r + 28:r + 284] = g["sgu_ln_w"][l]
        prow[0, r + 284:r + 540] = g["sgu_ln_b"][l]
    pfm[:, 2 * PFM_L:2 * PFM_L + 8] = g["final_norm"].reshape(8, 128).T
    crow = np.ascontiguousarray(g["conv_b"].reshape(1, 2048)).astype(np.float32)
    sgub = np.ascontiguousarray(g["sgu_b"].transpose(1, 0, 2)).astype(np.float32)
    bucket, valid = rel_bucket_band()
    tb = g["rel_bias"][bucket]
    tb = np.where(valid[:, :, None], tb, np.float32(NEG)).astype(np.float32)
    t5b = np.ascontiguousarray(tb.transpose(0, 2, 1))
    return dict(cst=host_consts(), pfm=pfm, prow=prow, crow=crow, sgub=sgub,
                sguw=np.ascontiguousarray(g["sgu_w"], dtype=np.float32), t5b=t5b)


_BIG = ["ffn1_w_gate", "ffn1_w_up", "ffn1_w_down", "ffn2_w_gate", "ffn2_w_up", "ffn2_w_down", "w_in", "w_out"]


def make_in_maps(inputs, ncores=8):
    small = _prep(inputs)
    big = {k: np.ascontiguousarray(np.asarray(inputs[k]), dtype=np.float32) for k in _BIG}
    xs = np.asarray(inputs["x"])
    maps = []
    for c in range(ncores):
        m = {"x": np.ascontiguousarray(xs[c], dtype=np.float32)}
        m.update(big)
        m.update(small)
        maps.append(m)
    return maps


_NC_CACHE = {}


def kernel(**inputs):
    if "nc" not in _NC_CACHE:
        _NC_CACHE["nc"] = build()[0]
    nc = _NC_CACHE["nc"]
    maps = make_in_maps(inputs, 8)
    res = run_bass_kernel_spmd(nc, maps, core_ids=list(range(8)))
    return np.stack([np.asarray(r["out"]) for r in res.results], axis=0).astype(np.float32)
```
